# Optimizing a Trainium2 kernel written in Bass

```python
import jax
import jax.numpy as jnp
from jax import lax
import numpy as np


D_MODEL = 1024
BATCH = 1
SEQ = 16384
DEPTH = 2

CHUNK = 64
D_MIX = D_MODEL
EPS = 1e-6
NEG_INF = -1e30

SGU_WIDTH = D_MIX // 4
SGU_HEADS = 4
SGU_HEAD_DIM = SGU_WIDTH // SGU_HEADS
SGU_BLOCK = 128

POOL_WIDTH = D_MIX // 4
POOL_WINDOWS = (2, 4, 8, 16)
POOL_GROUPS = len(POOL_WINDOWS)
POOL_GROUP_DIM = POOL_WIDTH // POOL_GROUPS

MLA_WIDTH = D_MIX // 2
MLA_HEADS = 4
V_HEAD_DIM = MLA_WIDTH // MLA_HEADS
QK_NOPE_DIM = 128
QK_ROPE_DIM = 64
QK_HEAD_DIM = QK_NOPE_DIM + QK_ROPE_DIM
Q_LORA_RANK = 384
KV_LORA_RANK = 256
ROPE_BASE = 10000.0
Q_BLOCK = 128

IN_SPLITS = (SGU_WIDTH, SGU_WIDTH, SGU_WIDTH, POOL_WIDTH, POOL_WIDTH, Q_LORA_RANK, KV_LORA_RANK, QK_ROPE_DIM, MLA_WIDTH)
D_IN = sum(IN_SPLITS)

kernel_name = "hybrid_sgu_pool_mla_block"


def rms_norm(x, g):
    xf = x.astype(jnp.float32)
    y = xf * lax.rsqrt(jnp.mean(xf * xf, axis=-1, keepdims=True) + EPS)
    return (y * g.astype(jnp.float32)).astype(x.dtype)


def layer_norm(x, g, b):
    xf = x.astype(jnp.float32)
    mu = jnp.mean(xf, axis=-1, keepdims=True)
    var = jnp.mean(jnp.square(xf - mu), axis=-1, keepdims=True)
    y = (xf - mu) * lax.rsqrt(var + EPS)
    return (y * g.astype(jnp.float32) + b.astype(jnp.float32)).astype(x.dtype)


def rope_tables(positions):
    inv_freq = ROPE_BASE ** (-jnp.arange(0, QK_ROPE_DIM, 2, dtype=jnp.float32) / QK_ROPE_DIM)
    ang = positions.astype(jnp.float32)[..., None] * inv_freq
    return jnp.cos(ang)[:, :, None, :], jnp.sin(ang)[:, :, None, :]


def apply_rope(x, cos, sin):
    xf = x.astype(jnp.float32)
    x1, x2 = jnp.split(xf, 2, axis=-1)
    return jnp.concatenate([x1 * cos - x2 * sin, x2 * cos + x1 * sin], axis=-1).astype(x.dtype)


def sgu_mixer(u, v, w_s, b_s, ln_g, ln_b):
    bsz, seq, _ = v.shape
    v = layer_norm(v, ln_g, ln_b)
    vb = v.reshape(bsz, seq // SGU_BLOCK, SGU_BLOCK, SGU_HEADS, SGU_HEAD_DIM)
    pos_chunk = jnp.arange(SGU_BLOCK) // CHUNK
    mask = (pos_chunk[None, :] <= pos_chunk[:, None]).astype(w_s.dtype)
    mixed = jnp.einsum("hij,bnjhd->bnihd", w_s * mask, vb) + b_s.T[None, None, :, :, None]
    return u * mixed.reshape(bsz, seq, SGU_WIDTH)


def pool_mixer(p, w_g, scale):
    bsz, seq, _ = p.shape
    pf = p.astype(jnp.float32)
    cs = jnp.concatenate([jnp.zeros((bsz, 1, POOL_WIDTH), jnp.float32), jnp.cumsum(pf, axis=1)], axis=1)
    t = jnp.arange(seq)
    outs = []
    for g, w in enumerate(POOL_WINDOWS):
        lo, hi = g * POOL_GROUP_DIM, (g + 1) * POOL_GROUP_DIM
        csg = cs[:, :, lo:hi]
        upper = csg[:, 1:]
        lower = jnp.concatenate([jnp.zeros((bsz, w - 1, POOL_GROUP_DIM), jnp.float32), csg[:, :seq + 1 - w]], axis=1)
        count = jnp.minimum(t + 1, w).astype(jnp.float32)[None, :, None]
        outs.append((upper - lower) / count - pf[:, :, lo:hi])
    pooled = jnp.stack(outs, axis=2).astype(p.dtype)
    mixed = jnp.einsum("bsgc,gcd->bsgd", pooled, w_g).reshape(bsz, seq, POOL_WIDTH)
    return mixed * scale


def mla_mixer(c_q, c_kv, k_rope, q_norm_g, w_uq, kv_norm_g, w_ukv, cos, sin):
    bsz, seq, _ = c_q.shape
    q = (rms_norm(c_q, q_norm_g) @ w_uq).reshape(bsz, seq, MLA_HEADS, QK_HEAD_DIM)
    q = jnp.concatenate([q[..., :QK_NOPE_DIM], apply_rope(q[..., QK_NOPE_DIM:], cos, sin)], axis=-1)
    kv = (rms_norm(c_kv, kv_norm_g) @ w_ukv).reshape(bsz, seq, MLA_HEADS, QK_NOPE_DIM + V_HEAD_DIM)
    k_nope, v = kv[..., :QK_NOPE_DIM], kv[..., QK_NOPE_DIM:]
    k_pe = apply_rope(k_rope[:, :, None, :], cos, sin)
    k = jnp.concatenate([k_nope, jnp.broadcast_to(k_pe, (bsz, seq, MLA_HEADS, QK_ROPE_DIM))], axis=-1)
    scale = QK_HEAD_DIM ** -0.5
    n_blk = seq // Q_BLOCK
    q_blocks = q.reshape(bsz, n_blk, Q_BLOCK, MLA_HEADS, QK_HEAD_DIM).transpose(1, 0, 2, 3, 4)
    key_chunk = jnp.arange(seq) // CHUNK

    def attend(args):
        q_blk, blk = args
        s = jnp.einsum("bqhd,bkhd->bhqk", q_blk, k).astype(jnp.float32) * scale
        q_chunk = (blk * Q_BLOCK + jnp.arange(Q_BLOCK)) // CHUNK
        allowed = key_chunk[None, :] <= q_chunk[:, None]
        s = jnp.where(allowed, s, NEG_INF)
        p = jax.nn.softmax(s, axis=-1).astype(v.dtype)
        return jnp.einsum("bhqk,bkhd->bqhd", p, v)

    o = lax.map(attend, (q_blocks, jnp.arange(n_blk)))
    return o.transpose(1, 0, 2, 3, 4).reshape(bsz, seq, MLA_WIDTH)


def hybrid_layer(x, pre_g, post_g, w_in, sgu_w, sgu_b, sgu_ln_g, sgu_ln_b, pool_w, pool_scale, q_norm_g, w_uq, kv_norm_g, w_ukv, w_out, cos, sin):
    h = rms_norm(x, pre_g)
    z = h @ w_in
    offs = [int(o) for o in np.cumsum(IN_SPLITS)[:-1]]
    sgu_u, sgu_v, sgu_gate, pool_in, pool_gate, c_q, c_kv, k_rope, mla_gate = jnp.split(z, offs, axis=-1)
    ya = sgu_mixer(sgu_u, sgu_v, sgu_w, sgu_b, sgu_ln_g, sgu_ln_b) * jax.nn.silu(sgu_gate)
    yb = pool_mixer(pool_in, pool_w, pool_scale) * jax.nn.silu(pool_gate)
    yc = mla_mixer(c_q, c_kv, k_rope, q_norm_g, w_uq, kv_norm_g, w_ukv, cos, sin) * jax.nn.silu(mla_gate)
    y = jnp.concatenate([ya, yb, yc], axis=-1) @ w_out
    return x + rms_norm(y, post_g)


def setup_inputs(seed: int = 0) -> dict:
    key = jax.random.key(seed)
    ks = jax.random.split(key, 16)

    def nrm(k, shape, s):
        return jax.random.normal(k, shape, jnp.float32) * s

    x = jax.random.normal(ks[0], (BATCH, SEQ, D_MODEL), jnp.float32)
    positions = jnp.broadcast_to(jnp.arange(SEQ, dtype=jnp.int32)[None, :], (BATCH, SEQ))
    pre_norm_g = 1.0 + nrm(ks[1], (DEPTH, D_MODEL), 0.02)
    post_norm_g = 1.0 + nrm(ks[2], (DEPTH, D_MODEL), 0.02)
    w_in = nrm(ks[3], (DEPTH, D_MODEL, D_IN), D_MODEL ** -0.5)
    sgu_w = nrm(ks[4], (DEPTH, SGU_HEADS, SGU_BLOCK, SGU_BLOCK), SGU_BLOCK ** -0.5)
    sgu_b = 1.0 + nrm(ks[5], (DEPTH, SGU_HEADS, SGU_BLOCK), 0.02)
    sgu_ln_g = 1.0 + nrm(ks[6], (DEPTH, SGU_WIDTH), 0.02)
    sgu_ln_b = nrm(ks[7], (DEPTH, SGU_WIDTH), 0.02)
    pool_w = nrm(ks[8], (DEPTH, POOL_GROUPS, POOL_GROUP_DIM, POOL_GROUP_DIM), POOL_GROUP_DIM ** -0.5)
    pool_scale = 1.0 + nrm(ks[9], (DEPTH, POOL_WIDTH), 0.02)
    q_norm_g = 1.0 + nrm(ks[10], (DEPTH, Q_LORA_RANK), 0.02)
    w_uq = nrm(ks[11], (DEPTH, Q_LORA_RANK, MLA_HEADS * QK_HEAD_DIM), Q_LORA_RANK ** -0.5)
    kv_norm_g = 1.0 + nrm(ks[12], (DEPTH, KV_LORA_RANK), 0.02)
    w_ukv = nrm(ks[13], (DEPTH, KV_LORA_RANK, MLA_HEADS * (QK_NOPE_DIM + V_HEAD_DIM)), KV_LORA_RANK ** -0.5)
    w_out = nrm(ks[14], (DEPTH, D_MIX, D_MODEL), D_MIX ** -0.5)
    return {"x": x, "positions": positions, "pre_norm_g": pre_norm_g, "post_norm_g": post_norm_g, "w_in": w_in, "sgu_w": sgu_w, "sgu_b": sgu_b, "sgu_ln_g": sgu_ln_g, "sgu_ln_b": sgu_ln_b, "pool_w": pool_w, "pool_scale": pool_scale, "q_norm_g": q_norm_g, "w_uq": w_uq, "kv_norm_g": kv_norm_g, "w_ukv": w_ukv, "w_out": w_out}


def reference(x, positions, pre_norm_g, post_norm_g, w_in, sgu_w, sgu_b, sgu_ln_g, sgu_ln_b, pool_w, pool_scale, q_norm_g, w_uq, kv_norm_g, w_ukv, w_out):
    cos, sin = rope_tables(positions)
    for l in range(DEPTH):
        x = hybrid_layer(x, pre_norm_g[l], post_norm_g[l], w_in[l], sgu_w[l], sgu_b[l], sgu_ln_g[l], sgu_ln_b[l], pool_w[l], pool_scale[l], q_norm_g[l], w_uq[l], kv_norm_g[l], w_ukv[l], w_out[l], cos, sin)
    return x
```

```python
import contextlib
import numpy as np
import ml_dtypes
import concourse.bass as bass
import concourse.mybir as mybir
from concourse.bass_utils import run_bass_kernel_spmd

F32 = mybir.dt.float32
BF16 = mybir.dt.bfloat16
I32 = mybir.dt.int32
U8 = mybir.dt.uint8
AF = mybir.ActivationFunctionType
ALU = mybir.AluOpType

NCORE = 8
D = 1024
SEQ = 16384
NT = 16
TOK = NT * 128
DIN = 2496
EPS = 1e-6
SCALE = 192 ** -0.5
BIG = 30000.0
DEBUG_STAGE = 99
KVC = 18432
TAILC = 17 * 256
O_SU, O_SV, O_SG, O_PI, O_PG, O_CQ, O_CKV, O_KR, O_MG = 0, 256, 512, 768, 1024, 1280, 1664, 1920, 1984
TWO_PI = 2.0 * np.pi


def _split_2pi():
    c1 = 6.28125
    rem = TWO_PI - c1
    bits = np.array([rem], np.float32).view(np.uint32)
    bits &= np.uint32(0xFFFFF000)
    c2 = float(bits.view(np.float32)[0])
    c3 = float(np.float32(rem - c2))
    return float(c1), c2, c3


C1, C2, C3 = _split_2pi()


class Op:
    __slots__ = ("eng", "fn", "reads", "writes", "dma", "waits", "signal", "ticket", "sem", "pre", "cc")

    def __init__(self, eng, fn, reads, writes, dma):
        self.eng = eng
        self.fn = fn
        self.reads = reads
        self.writes = writes
        self.dma = dma
        self.waits = []
        self.signal = False
        self.ticket = None
        self.sem = None
        self.pre = None
        self.cc = False


class Prog:
    NDMA = 8

    def __init__(self):
        self.ops = []
        self.queues = {q: [] for q in ("pe", "act", "dve", "pool", "sp")}
        self.barriers = []

    def add(self, eng, fn, reads=(), writes=()):
        op = Op(eng, fn, tuple(reads), tuple(writes), False)
        self.ops.append(op)
        self.queues[eng].append(op)
        return op

    def dma(self, q, fn, reads=(), writes=()):
        op = Op(q, fn, tuple(reads), tuple(writes), True)
        self.ops.append(op)
        self.queues[q].append(op)
        return op

    def cc(self, fn, reads=(), writes=()):
        op = Op("pool", fn, tuple(reads), tuple(writes), True)
        op.cc = True
        self.ops.append(op)
        self.queues["pool"].append(op)
        return op

    def barrier(self):
        self.barriers.append(len(self.ops))

    def resolve(self):
        last_w = {}
        readers = {}
        deps = {}
        bars = set(self.barriers)
        last_compute = {}
        dmas_since = []
        pending = {q: set() for q in self.queues}
        for idx, op in enumerate(self.ops):
            if idx in bars:
                B = set(last_compute.values()) | set(dmas_since)
                for q in pending:
                    pending[q] |= B
                dmas_since = []
            d = set()
            for r in op.reads:
                w = last_w.get(r)
                if w is not None:
                    d.add(w)
            for r in op.writes:
                w = last_w.get(r)
                if w is not None:
                    d.add(w)
                for rd in readers.get(r, ()):
                    d.add(rd)
            if pending[op.eng]:
                d |= pending[op.eng]
                pending[op.eng] = set()
            d.discard(op)
            for r in op.reads:
                readers.setdefault(r, []).append(op)
            for r in op.writes:
                last_w[r] = op
                readers[r] = []
            deps[op] = d
            if op.dma:
                if not op.cc:
                    dmas_since.append(op)
            else:
                last_compute[op.eng] = op
        pos = {}
        for q, lst in self.queues.items():
            for i, op in enumerate(lst):
                pos[op] = i
        needed = {}
        for op, d in deps.items():
            nd = set()
            bestq = {}
            for p in d:
                if p.dma:
                    nd.add(p)
                    continue
                if p.eng == op.eng and not op.dma and p.eng == "pe":
                    continue
                b = bestq.get(p.eng)
                if b is None or pos[p] > pos[b]:
                    bestq[p.eng] = p
            nd |= set(bestq.values())
            needed[op] = nd
            for p in nd:
                p.signal = True
        for q, lst in self.queues.items():
            c = 0
            k = 0
            ncc = 0
            for op in lst:
                if op.cc:
                    op.sem = ("cc", ncc)
                    op.ticket = 1
                    ncc += 1
                elif op.dma:
                    s = k % self.NDMA
                    j = k // self.NDMA + 1
                    op.sem = ("dma", q, s)
                    op.ticket = 16 * j
                    if j > 1:
                        op.pre = (op.sem, 16 * (j - 1))
                    k += 1
                elif op.signal:
                    c += 1
                    op.sem = ("eng", q)
                    op.ticket = c
        for q, lst in self.queues.items():
            seen = {}
            for op in lst:
                if op.pre is not None and seen.get(op.pre[0], 0) < op.pre[1]:
                    seen[op.pre[0]] = op.pre[1]
                    op.waits.append(op.pre)
                best = {}
                for p in needed[op]:
                    if best.get(p.sem, 0) < p.ticket:
                        best[p.sem] = p.ticket
                for s, v in best.items():
                    if seen.get(s, 0) < v:
                        seen[s] = v
                        op.waits.append((s, v))

    def emit(self, nc):
        semkeys = set()
        for op in self.ops:
            if op.sem is not None and (op.dma or op.signal):
                semkeys.add(op.sem)
        with contextlib.ExitStack() as st:
            sems = {}
            for k in sorted(semkeys, key=str):
                sems[k] = st.enter_context(nc.semaphore("s_" + "_".join(str(x) for x in k)))
            block = st.enter_context(nc.Block())
            reg = {"pe": block.tensor, "act": block.scalar, "dve": block.vector,
                   "pool": block.gpsimd, "sp": block.sync}
            for q, lst in self.queues.items():
                if not lst:
                    continue

                def body(e, lst=lst):
                    for op in lst:
                        for (s, v) in op.waits:
                            e.wait_ge(sems[s], v)
                        ins = op.fn(e)
                        if op.cc:
                            ins.then_inc(sems[op.sem])
                        elif op.dma:
                            ins.then_inc(sems[op.sem], 16)
                        elif op.signal:
                            ins.then_inc(sems[op.sem], 1)
                    fin = {}
                    for op in lst:
                        if op.dma:
                            fin[op.sem] = max(fin.get(op.sem, 0), op.ticket)
                    for s, v in fin.items():
                        e.wait_ge(sems[s], v)

                reg[q](body)


class Arena:
    def __init__(self, t, nbytes):
        self.t = t
        self.n = nbytes
        self.off = 0

    def reset(self):
        self.off = 0

    def alloc(self, shape, dt):
        esz = {F32: 4, BF16: 2, I32: 4}[dt]
        n = int(np.prod(shape)) * esz
        n_al = (n + 63) // 64 * 64
        assert self.off + n_al <= self.n, ("arena overflow", self.off, n_al, self.n)
        ap = self.t[:, self.off:self.off + n].bitcast(dt)
        self.off += n_al
        if len(shape) == 2:
            ap = ap.rearrange("p (a b) -> p a b", b=shape[1])
        elif len(shape) == 3:
            ap = ap.rearrange("p (a b c) -> p a b c", b=shape[1], c=shape[2])
        return ap


class Builder:
    def __init__(self, nc, st):
        self.nc = nc
        self.st = st
        self.P = Prog()
        self.dq = 0
        self.pay_keys = []
        self.tail_keys = []
        self.state = {"nalloc": 0}
        self.xin_key = "xin"
        self.xout_key = "xout"
        nc_ = nc

        def sb(name, shape, dt):
            return st.enter_context(nc_.sbuf_tensor(name, shape, dt))

        self.sb = sb
        self.catT = sb("catT", [128, 8, TOK], BF16)
        self.QN = sb("QN", [128, 4, TOK], BF16)
        self.QPE = sb("QPE", [128, 4, TOK], BF16)
        self.ident = sb("ident_sb", [128, 128], BF16)
        self.ones = sb("ones_sb", [128, 128], BF16)
        self.zeros = sb("zeros_sb", [128, 256], F32)
        self.invf = sb("invf_sb", [64, 1], F32)
        self.ss = sb("ss", [128, 4], F32)
        self.rs = sb("rs", [128, 4], F32)
        ARN = 134 * 1024
        self.arena_t = sb("arena", [128, ARN], U8)
        self.ar = Arena(self.arena_t, ARN)
        self.ps = [st.enter_context(nc.psum_tensor("psb%d" % b, [128, 512], F32)) for b in range(8)]

    def fork(self):
        import copy
        return copy.copy(self)

    def dmaq(self):
        if getattr(self, "sp_only", False):
            return "sp"
        self.dq ^= 1
        return "sp" if self.dq else "pool"

    def load_consts(self, C):
        P = self.P
        P.dma("sp", lambda e: e.dma_start(out=self.ident[:], in_=C["ident"]), writes=["ident"])
        P.dma("sp", lambda e: e.dma_start(out=self.invf[:], in_=C["inv_freq"]), writes=["invf"])
        P.add("pool", lambda e: e.memset(self.ones[:], 1.0), writes=["ones"])
        P.add("pool", lambda e: e.memset(self.zeros[:], 0.0), writes=["zeros"])

    def pk(self, lst):
        k = ("pay", len(self.pay_keys) + len(self.tail_keys))
        lst.append(k)
        return k

    def alloc_front(self, mode):
        a = self.ar
        if self.state["nalloc"] > 0:
            self.P.barrier()
        self.state["nalloc"] += 1
        a.reset()
        self.mode = mode
        if mode == "A":
            cols = [("pi", 256), ("ckv", 256), ("kr", 64), ("krot", 64)]
        else:
            cols = [("su", 256), ("sv", 256), ("sg", 256), ("pg", 256), ("pi", 256), ("cq", 384), ("mg", 512)]
        self.wcol = {}
        o = 0
        for n_, w_ in cols:
            self.wcol[n_] = (o, w_)
            o += w_
        self.WIC = o
        self.WI = a.alloc([8, o], BF16)
        self.wst = [a.alloc([DIN], F32), a.alloc([DIN], F32)]
        self.preg = a.alloc([8], F32)
        self.npreg = a.alloc([8], F32)
        self.xt = [a.alloc([D], F32), a.alloc([D], F32)]
        self.hb = a.alloc([D], BF16)
        self.hT = a.alloc([8, 512], BF16)
        self.cosT = a.alloc([512], F32)
        self.sinT = a.alloc([512], F32)
        self.posi = a.alloc([512], I32)
        self.rt = [a.alloc([512], F32) for _ in range(4)]
        self.cT = a.alloc([3, 512], F32)
        self.cn = a.alloc([3, 512], BF16)
        self.rstdT = a.alloc([512], F32)
        self.pT = a.alloc([2, 512], BF16)
        self.wg = a.alloc([2, 128], BF16)
        self.wgst = a.alloc([2, 64], F32)
        self.gq = a.alloc([4], F32)
        self.ngq = a.alloc([4], F32)
        if mode == "A":
            self.WKV = a.alloc([2, 1024], BF16)
            self.kts = a.alloc([4, 512], BF16)
            self.vs = a.alloc([4, 4, 128], BF16)
            self.kpes = a.alloc([512], BF16)
            self.pws = a.alloc([NT, 256], BF16)
            self.ztail = a.alloc([256], BF16)
        else:
            self.WQ = a.alloc([3, 1024], BF16)
            self.wmT = a.alloc([4, 128], BF16)
            self.lng = a.alloc([256], F32)
            self.lnb = a.alloc([256], F32)
            self.psc = a.alloc([256], F32)
            self.bsT = a.alloc([4], F32)
            self.bsf = a.alloc([256], F32)
            self.bandM = a.alloc([4, 128], BF16)
            self.bandF = a.alloc([4, 128], BF16)
            self.bandH = a.alloc([4, 128], BF16)
            self.halo = a.alloc([NT, 256], BF16)
            self.pw1 = a.alloc([256], BF16)
            self.vn = a.alloc([256], F32)
            self.vln = a.alloc([256], BF16)
            self.sg = a.alloc([512], F32)
            self.m1 = a.alloc([256], F32)
            self.ya = a.alloc([256], BF16)
            self.yb = a.alloc([256], BF16)
            self.y1 = a.alloc([256], F32)
            self.st6 = a.alloc([8], F32)
            self.wmask_sb = a.alloc([512], BF16)

    def prep_w_in(self, W):
        P = self.P
        P.dma("sp", lambda e: e.dma_start(out=self.preg[:], in_=W["pre_g"]), writes=["preg"])
        P.add("dve", lambda e: e.tensor_scalar(out=self.npreg[:], in0=self.preg[:], scalar1=-1.0, scalar2=1.0,
                                               op0=ALU.mult, op1=ALU.mult), reads=["preg"], writes=["npreg"])
        src = {"su": O_SU, "sv": O_SV, "sg": O_SG, "pg": O_PG, "pi": O_PI, "cq": O_CQ, "ckv": O_CKV, "kr": O_KR,
               "mg": O_MG}
        k = 0
        for c in range(8):
            stg = self.wst[c % 2]
            sk = ("wst", c % 2)
            P.dma(self.dmaq(), lambda e, c=c, stg=stg: e.dma_start(out=stg[:], in_=W["w_in"][c * 128:(c + 1) * 128, :]),
                  writes=[sk])
            for name, (o, w_) in self.wcol.items():
                pieces = []
                if name == "krot":
                    pieces.append((o, O_KR + 32, 32, True))
                    pieces.append((o + 32, O_KR, 32, False))
                else:
                    pieces.append((o, src[name], w_, False))
                for (d0, s0, n_, neg) in pieces:
                    g = self.npreg if neg else self.preg
                    gk = "npreg" if neg else "preg"
                    if k % 2 == 0:
                        P.add("dve", lambda e, c=c, d0=d0, s0=s0, n_=n_, g=g, stg=stg: e.tensor_scalar(
                            out=self.WI[:, c, d0:d0 + n_], in0=stg[:, s0:s0 + n_], scalar1=g[:, c:c + 1], scalar2=1.0,
                            op0=ALU.mult, op1=ALU.mult), reads=[sk, gk], writes=[("WI", c)])
                    else:
                        P.add("act", lambda e, c=c, d0=d0, s0=s0, n_=n_, g=g, stg=stg: e.activation(
                            out=self.WI[:, c, d0:d0 + n_], in_=stg[:, s0:s0 + n_], func=AF.Copy, scale=g[:, c:c + 1]),
                            reads=[sk, gk], writes=[("WI", c)])
                    k += 1

    def prep_pool_w(self, W):
        P = self.P
        P.dma("sp", lambda e: e.dma_start(out=self.wgst[:], in_=W["pool_wp"]), writes=["wgst"])
        P.add("dve", lambda e: e.memset(self.wg[:], 0.0), writes=["wg"])
        P.add("dve", lambda e: e.tensor_copy(out=self.wg[0:64, :, 0:64], in_=self.wgst[0:64, :, :]), reads=["wgst"],
              writes=["wg"])
        P.add("dve", lambda e: e.tensor_copy(out=self.wg[64:128, :, 64:128], in_=self.wgst[64:128, :, :]),
              reads=["wgst"], writes=["wg"])

    def prep_w_ukv(self, W):
        P = self.P
        P.dma("sp", lambda e: e.dma_start(out=self.gq[:, 0:2], in_=W["kv_g"]), writes=["gq"])
        for c in range(2):
            stg = self.wst[c % 2]
            sk = ("wst", c % 2)
            P.dma(self.dmaq(), lambda e, c=c, stg=stg: e.dma_start(out=stg[:, 0:1024], in_=W["w_ukv"][c * 128:(c + 1) * 128, :]),
                  writes=[sk])
            sv = stg[:, 0:1024].rearrange("p (h t d) -> p h t d", h=4, t=2)
            for t in range(2):
                dv = self.WKV[:, c, t * 512:(t + 1) * 512].rearrange("p (h d) -> p h d", h=4)
                P.add("dve", lambda e, c=c, t=t, sv=sv, dv=dv: e.tensor_scalar(
                    out=dv, in0=sv[:, :, t, :], scalar1=self.gq[:, c:c + 1], scalar2=1.0, op0=ALU.mult, op1=ALU.mult),
                    reads=[sk, "gq"], writes=[("WKV", c)])

    def prep_w_uq(self, W):
        P = self.P
        P.dma("sp", lambda e: e.dma_start(out=self.gq[:, 0:3], in_=W["q_g"]), writes=["gq"])
        P.add("dve", lambda e: e.tensor_scalar(out=self.ngq[:, 0:3], in0=self.gq[:, 0:3], scalar1=-1.0, scalar2=1.0,
                                               op0=ALU.mult, op1=ALU.mult), reads=["gq"], writes=["ngq"])
        for c in range(3):
            stg = self.wst[c % 2]
            sk = ("wst", c % 2)
            P.dma(self.dmaq(), lambda e, c=c, stg=stg: e.dma_start(out=stg[:, 0:768], in_=W["w_uq"][c * 128:(c + 1) * 128, :]),
                  writes=[sk])
            P.add("dve", lambda e, c=c, stg=stg: e.tensor_scalar(
                out=self.WQ[:, c, 0:768], in0=stg[:, 0:768], scalar1=self.gq[:, c:c + 1], scalar2=1.0,
                op0=ALU.mult, op1=ALU.mult), reads=[sk, "gq"], writes=[("WQ", c)])
            sv = stg[:, 0:768].rearrange("p (h d) -> p h d", h=4)
            dv = self.WQ[:, c, 768:1024].rearrange("p (h d) -> p h d", h=4)
            P.add("dve", lambda e, c=c, sv=sv, dv=dv: e.tensor_scalar(
                out=dv[:, :, 0:32], in0=sv[:, :, 160:192], scalar1=self.ngq[:, c:c + 1], scalar2=1.0,
                op0=ALU.mult, op1=ALU.mult), reads=[sk, "ngq"], writes=[("WQ", c)])
            P.add("dve", lambda e, c=c, sv=sv, dv=dv: e.tensor_scalar(
                out=dv[:, :, 32:64], in0=sv[:, :, 128:160], scalar1=self.gq[:, c:c + 1], scalar2=1.0,
                op0=ALU.mult, op1=ALU.mult), reads=[sk, "gq"], writes=[("WQ", c)])

    def prep_sgu(self, W, C):
        P = self.P
        stg = self.wst[0]
        sv = stg[:, 0:512].rearrange("p (h i) -> p h i", h=4)
        P.dma("sp", lambda e: e.dma_start(out=sv, in_=W["sgu_wT"].rearrange("h j i -> j h i")), writes=[("wst", 0)])
        P.add("dve", lambda e: e.tensor_copy(out=self.wmT[:], in_=sv), reads=[("wst", 0)], writes=["wmT"])
        P.add("dve", lambda e: e.memset(self.wmT[64:128, :, 0:64], 0.0), writes=["wmT"])
        P.dma("sp", lambda e: e.dma_start(out=self.bsT[:], in_=W["sgu_bT"]), writes=["bsT"])
        for h in range(4):
            P.add("dve", lambda e, h=h: e.tensor_scalar(out=self.bsf[:, 64 * h:64 * h + 64], in0=self.zeros[:, 0:64],
                                                        scalar1=self.bsT[:, h:h + 1], scalar2=1.0, op0=ALU.add,
                                                        op1=ALU.mult), reads=["bsT", "zeros"], writes=["bsf"])
        P.dma("sp", lambda e: e.dma_start(out=self.lng[:], in_=W["ln_g"].partition_broadcast(128)), writes=["lng"])
        P.dma("sp", lambda e: e.dma_start(out=self.lnb[:], in_=W["ln_b"].partition_broadcast(128)), writes=["lnb"])
        P.dma("sp", lambda e: e.dma_start(out=self.psc[:], in_=W["pool_scale"].partition_broadcast(128)), writes=["psc"])
        P.dma("sp", lambda e: e.dma_start(out=self.bandM[:], in_=C["bandM"]), writes=["bandM"])
        P.dma("sp", lambda e: e.dma_start(out=self.bandF[:], in_=C["bandF"]), writes=["bandF"])
        P.dma("sp", lambda e: e.dma_start(out=self.bandH[:], in_=C["bandH"]), writes=["bandH"])

    def rope_tables(self, j, pos):
        P = self.P
        t0, t1, t2, t3 = self.rt
        P.dma("sp", lambda e: e.dma_start(out=self.posi[0:64, :], in_=pos[j * 512:(j + 1) * 512].partition_broadcast(64)),
              writes=["posi"])
        v = lambda a: a[0:64, :]
        P.add("dve", lambda e: e.tensor_copy(out=v(t0), in_=self.posi[0:64, :]), reads=["posi"], writes=["rt0"])
        P.add("dve", lambda e: e.tensor_scalar(out=v(t0), in0=v(t0), scalar1=self.invf[:, 0:1], scalar2=1.0,
                                               op0=ALU.mult, op1=ALU.mult), reads=["rt0", "invf"], writes=["rt0"])
        MAG = 12582912.0
        P.add("dve", lambda e: e.tensor_scalar(out=v(t1), in0=v(t0), scalar1=float(1.0 / TWO_PI), scalar2=MAG,
                                               op0=ALU.mult, op1=ALU.add), reads=["rt0"], writes=["rt1"])
        P.add("dve", lambda e: e.tensor_scalar(out=v(t1), in0=v(t1), scalar1=-MAG, scalar2=1.0,
                                               op0=ALU.add, op1=ALU.mult), reads=["rt1"], writes=["rt1"])
        P.add("dve", lambda e: e.scalar_tensor_tensor(out=v(t2), in0=v(t1), scalar=-C1, in1=v(t0), op0=ALU.mult,
                                                      op1=ALU.add), reads=["rt1", "rt0"], writes=["rt2"])
        P.add("dve", lambda e: e.scalar_tensor_tensor(out=v(t3), in0=v(t1), scalar=-C2, in1=v(t2), op0=ALU.mult,
                                                      op1=ALU.add), reads=["rt1", "rt2"], writes=["rt3"])
        P.add("dve", lambda e: e.scalar_tensor_tensor(out=v(t2), in0=v(t1), scalar=-C3, in1=v(t3), op0=ALU.mult,
                                                      op1=ALU.add), reads=["rt1", "rt3"], writes=["rt2"])
        LIM = 3.14159
        P.add("dve", lambda e: e.tensor_scalar(out=v(t2), in0=v(t2), scalar1=LIM, scalar2=-LIM, op0=ALU.min,
                                               op1=ALU.max), reads=["rt2"], writes=["rt2"])
        P.add("act", lambda e: e.activation(out=self.sinT[0:64, :], in_=v(t2), func=AF.Sin), reads=["rt2"],
              writes=["sinT"])
        P.add("dve", lambda e: e.scalar_tensor_tensor(out=v(t3), in0=v(t2), scalar=-1.0, in1=v(t2), op0=ALU.mult,
                                                      op1=ALU.max), reads=["rt2"], writes=["rt3"])
        P.add("dve", lambda e: e.tensor_scalar(out=v(t3), in0=v(t3), scalar1=-1.0, scalar2=float(np.pi / 2),
                                               op0=ALU.mult, op1=ALU.add), reads=["rt3"], writes=["rt3"])
        P.add("act", lambda e: e.activation(out=self.cosT[0:64, :], in_=v(t3), func=AF.Sin), reads=["rt3"],
              writes=["cosT"])

    def emit_hT(self, j, xin):
        P = self.P
        psT = self.ps[0][:].bitcast(BF16).rearrange("p (c t) -> p c t", c=8)
        for r in range(4):
            m = 4 * j + r
            xb = self.xt[m % 2]
            xk = ("xt", m % 2)
            P.dma("sp", lambda e, m=m, xb=xb: e.dma_start(out=xb[:], in_=xin[m * 128:(m + 1) * 128, :]),
                  reads=[(self.xin_key, m)], writes=[xk])
            P.add("act", lambda e, xb=xb: e.activation(out=self.hb[:], in_=xb[:], func=AF.Square,
                                                       accum_out=self.ss[:, 0:1]), reads=[xk], writes=["hb", "ss0"])
            P.add("dve", lambda e: e.tensor_scalar(out=self.rs[:, 0:1], in0=self.ss[:, 0:1], scalar1=1.0 / D, scalar2=EPS,
                                                   op0=ALU.mult, op1=ALU.add), reads=["ss0"], writes=["rs0"])
            P.add("act", lambda e: e.activation(out=self.rs[:, 0:1], in_=self.rs[:, 0:1], func=AF.Ln), reads=["rs0"],
                  writes=["rs0"])
            P.add("act", lambda e: e.activation(out=self.rs[:, 0:1], in_=self.rs[:, 0:1], func=AF.Exp, scale=-0.5),
                  reads=["rs0"], writes=["rs0"])
            P.add("act", lambda e, xb=xb: e.activation(out=self.hb[:], in_=xb[:], func=AF.Copy, scale=self.rs[:, 0:1]),
                  reads=[xk, "rs0"], writes=["hb"])
            for c in range(8):
                P.add("pe", lambda e, c=c: e.transpose(out=psT[:, c, :], in_=self.hb[:, c * 128:(c + 1) * 128],
                                                       identity=self.ident[:]), reads=["hb", "ident"], writes=["ps0"])
            P.add("dve", lambda e, r=r: e.tensor_copy(out=self.hT[:, :, r * 128:(r + 1) * 128], in_=psT),
                  reads=["ps0"], writes=[("hT", r)])

    def fm_matmul(self, ps, pk, name, sub, M=128, m0=0):
        P = self.P
        o, _ = self.wcol[name]
        col = o + sub
        for c in range(8):
            P.add("pe", lambda e, c=c: e.matmul(ps[0:M, :], lhsT=self.WI[:, c, col:col + M], rhs=self.hT[:, c, :],
                                                start=(c == 0), stop=(c == 7)),
                  reads=[("WI", c)] + [("hT", r) for r in range(4)], writes=[pk])

    def rms_fm(self, nch, n):
        P = self.P
        for c in range(nch):
            P.add("act", lambda e, c=c: e.activation(out=self.cn[:, c, :], in_=self.cT[:, c, :], func=AF.Square),
                  reads=[("cT", c)], writes=[("cn", c)])
        for c in range(nch):
            P.add("pe", lambda e, c=c: e.matmul(self.ps[6][:], lhsT=self.ones[:], rhs=self.cn[:, c, :], start=(c == 0),
                                                stop=(c == nch - 1)), reads=[("cn", c), "ones"], writes=["ps6"])
        P.add("dve", lambda e: e.tensor_scalar(out=self.rstdT[:], in0=self.ps[6][:], scalar1=1.0 / n, scalar2=EPS,
                                               op0=ALU.mult, op1=ALU.add), reads=["ps6"], writes=["rstdT"])
        P.add("act", lambda e: e.activation(out=self.rstdT[:], in_=self.rstdT[:], func=AF.Ln), reads=["rstdT"],
              writes=["rstdT"])
        P.add("act", lambda e: e.activation(out=self.rstdT[:], in_=self.rstdT[:], func=AF.Exp, scale=-0.5),
              reads=["rstdT"], writes=["rstdT"])
        for c in range(nch):
            P.add("dve", lambda e, c=c: e.tensor_tensor(out=self.cn[:, c, :], in0=self.cT[:, c, :], in1=self.rstdT[:],
                                                        op=ALU.mult), reads=[("cT", c), "rstdT"], writes=[("cn", c)])

    def pool_pw(self, j, dst_of_tile):
        P = self.P
        for s in range(2):
            ps = self.ps[4 + s]
            pk = "ps%d" % (4 + s)
            self.fm_matmul(ps, pk, "pi", 128 * s)
            P.add("act", lambda e, s=s, ps=ps: e.activation(out=self.pT[:, s, :], in_=ps[:], func=AF.Copy), reads=[pk],
                  writes=[("pT", s)])
        for r in range(4):
            for ch in range(2):
                P.add("pe", lambda e, r=r, ch=ch: e.matmul(
                    self.ps[7][:, 128 * ch:128 * ch + 128], lhsT=self.pT[:, ch, r * 128:(r + 1) * 128],
                    rhs=self.wg[:, ch, :], start=True, stop=True), reads=[("pT", ch), "wg"], writes=["ps7"])
            dst, dk = dst_of_tile(r)
            P.add("dve", lambda e, dst=dst: e.tensor_copy(out=dst, in_=self.ps[7][:, 0:256]), reads=["ps7"], writes=[dk])

    def phase_A(self, W, C, xin, pos, kv_pay, tail_pay):
        P = self.P
        self.alloc_front("A")
        self.prep_w_in(W)
        self.prep_w_ukv(W)
        self.prep_pool_w(W)
        if DEBUG_STAGE == 0:
            return
        P.add("pool", lambda e: e.memset(self.ztail[:], 0.0), writes=["ztail"])
        P.dma("pool", lambda e: e.dma_start(out=tail_pay[:, 0:256], in_=self.ztail[0:16, :]), reads=["ztail"],
              writes=[self.pk(self.tail_keys)])
        for j in range(4):
            self.emit_hT(j, xin)
            if DEBUG_STAGE == 1:
                continue
            self.rope_tables(j, pos)
            if DEBUG_STAGE == 2:
                continue
            for s in range(2):
                ps = self.ps[4 + s]
                pk = "ps%d" % (4 + s)
                self.fm_matmul(ps, pk, "ckv", 128 * s)
                P.add("act", lambda e, s=s, ps=ps: e.activation(out=self.cT[:, s, :], in_=ps[:], func=AF.Copy),
                      reads=[pk], writes=[("cT", s)])
            self.rms_fm(2, 256)
            if DEBUG_STAGE == 3:
                continue
            for h in range(4):
                ps = self.ps[4 + h % 2]
                pk = "ps%d" % (4 + h % 2)
                for c in range(2):
                    P.add("pe", lambda e, h=h, c=c, ps=ps: e.matmul(ps[:], lhsT=self.WKV[:, c, 128 * h:128 * h + 128],
                                                                   rhs=self.cn[:, c, :], start=(c == 0), stop=(c == 1)),
                          reads=[("WKV", c), ("cn", c)], writes=[pk])
                eng = "act" if h % 2 == 0 else "dve"
                if eng == "act":
                    P.add("act", lambda e, h=h, ps=ps: e.activation(out=self.kts[:, h, :], in_=ps[:], func=AF.Copy),
                          reads=[pk], writes=[("kts", h)])
                else:
                    P.add("dve", lambda e, h=h, ps=ps: e.tensor_copy(out=self.kts[:, h, :], in_=ps[:]), reads=[pk],
                          writes=[("kts", h)])
            dst = kv_pay[:, 0:8192].rearrange("p (h g c) -> p h g c", h=4, g=4)[:, :, j, :]
            P.dma("sp", lambda e, dst=dst: e.dma_start(out=dst, in_=self.kts[:]), reads=[("kts", h) for h in range(4)],
                  writes=[self.pk(self.pay_keys)])
            if DEBUG_STAGE == 4:
                continue
            for r in range(4):
                ps = self.ps[2 + r % 2]
                pk = "ps%d" % (2 + r % 2)
                for c in range(2):
                    P.add("pe", lambda e, r=r, c=c, ps=ps: e.matmul(ps[:], lhsT=self.cn[:, c, r * 128:(r + 1) * 128],
                                                                   rhs=self.WKV[:, c, 512:1024], start=(c == 0),
                                                                   stop=(c == 1)),
                          reads=[("WKV", c), ("cn", c)], writes=[pk])
                dstv = self.vs[:, r, :, :]
                if r % 2 == 0:
                    P.add("act", lambda e, ps=ps, dstv=dstv: e.activation(
                        out=dstv, in_=ps[:].rearrange("p (h d) -> p h d", h=4), func=AF.Copy), reads=[pk],
                        writes=[("vs", r)])
                else:
                    P.add("dve", lambda e, ps=ps, dstv=dstv: e.tensor_copy(
                        out=dstv, in_=ps[:].rearrange("p (h d) -> p h d", h=4)), reads=[pk], writes=[("vs", r)])
            for h in range(4):
                dst = kv_pay[:, 8192 + h * 2048 + j * 512: 8192 + h * 2048 + (j + 1) * 512].rearrange(
                    "p (r d) -> p r d", r=4)
                P.dma("pool", lambda e, dst=dst, h=h: e.dma_start(out=dst, in_=self.vs[:, :, h, :]),
                      reads=[("vs", r) for r in range(4)], writes=[self.pk(self.pay_keys)])
            if DEBUG_STAGE == 5:
                continue
            self.fm_matmul(self.ps[2], "ps2", "kr", 0, M=64)
            self.fm_matmul(self.ps[3], "ps3", "krot", 0, M=64)
            self.rope_combine(self.ps[2], "ps2", self.ps[3], "ps3", self.kpes[0:64, :], "kpes")
            P.dma("sp", lambda e, j=j: e.dma_start(out=kv_pay[0:64, 16384 + j * 512:16384 + (j + 1) * 512],
                                                   in_=self.kpes[0:64, :]), reads=["kpes"],
                  writes=[self.pk(self.pay_keys)])
            if DEBUG_STAGE == 6:
                continue
            self.pool_pw(j, lambda r, j=j: (self.pws[:, 4 * j + r, :], ("pws", 4 * j + r)))
        dst = tail_pay[:, 256:TAILC].rearrange("s (m c) -> s m c", m=NT)
        P.dma("sp", lambda e: e.dma_start(out=dst, in_=self.pws[112:128, :, :]),
              reads=[("pws", m) for m in range(NT)], writes=[self.pk(self.tail_keys)])

    def allgather(self, kv_pay, tail_pay, kv_all2, tail_all2):
        P = self.P
        rg = [list(range(NCORE))]
        P.cc(lambda e: e.collective_compute("AllGather", ALU.bypass, replica_groups=rg, ins=[tail_pay],
                                            outs=[tail_all2]), reads=list(self.tail_keys), writes=["tail_all"])
        P.cc(lambda e: e.collective_compute("AllGather", ALU.bypass, replica_groups=rg, ins=[kv_pay],
                                            outs=[kv_all2]), reads=list(self.pay_keys), writes=["kv_all"])
        self.tail_keys.clear()
        self.pay_keys.clear()

    def rope_combine(self, psa, ka, psb, kb, dst, dk):
        P = self.P
        t0, t1 = self.rt[0], self.rt[1]
        P.add("dve", lambda e: e.tensor_tensor(out=t0[0:64, :], in0=psa[0:64, :], in1=self.cosT[0:64, :], op=ALU.mult),
              reads=[ka, "cosT"], writes=["rt0"])
        P.add("dve", lambda e: e.tensor_tensor(out=t1[0:64, :], in0=psb[0:64, :], in1=self.sinT[0:64, :], op=ALU.mult),
              reads=[kb, "sinT"], writes=["rt1"])
        P.add("dve", lambda e: e.tensor_tensor(out=dst, in0=t0[0:64, :], in1=t1[0:64, :], op=ALU.add),
              reads=["rt0", "rt1"], writes=[dk])

    def phase_B_front(self, W, C, xin, pos, tail_all):
        P = self.P
        self.sp_only = True
        self.alloc_front("B")
        self.prep_w_in(W)
        self.prep_w_uq(W)
        self.prep_pool_w(W)
        self.prep_sgu(W, C)
        P.dma("sp", lambda e: e.dma_start(out=self.halo[0:16, :, :],
                                          in_=tail_all[7, :, 0:NT * 256].rearrange("s (m c) -> s m c", m=NT)),
              reads=["tail_all"], writes=["halo"])
        for b in range(1, 8):
            P.dma(self.dmaq(), lambda e, b=b: e.dma_start(
                out=self.halo[16 * b:16 * b + 16, :, :],
                in_=tail_all[b - 1, :, 256:TAILC].rearrange("s (m c) -> s m c", m=NT)), reads=["tail_all"],
                writes=["halo"])
        P.dma("sp", lambda e: e.dma_start(out=self.wmask_sb[64:72, :], in_=C["wmask"]), writes=["wmask"])
        for h in range(4):
            for j in range(4):
                P.add("dve", lambda e, h=h, j=j: e.tensor_copy(out=self.QPE[64:72, h, j * 512:(j + 1) * 512],
                                                                in_=self.wmask_sb[64:72, :]),
                      reads=["wmask"], writes=[("QPE", h, j)])
        psT2 = self.ps[1][:].bitcast(BF16).rearrange("p (c t) -> p c t", c=8)
        psT5 = self.ps[5][:].bitcast(BF16).rearrange("p (c t) -> p c t", c=8)
        for j in range(4):
            self.emit_hT(j, xin)
            self.rope_tables(j, pos)
            for s in range(3):
                ps = self.ps[4 + s % 2]
                pk = "ps%d" % (4 + s % 2)
                self.fm_matmul(ps, pk, "cq", 128 * s)
                P.add("act", lambda e, s=s, ps=ps: e.activation(out=self.cT[:, s, :], in_=ps[:], func=AF.Copy),
                      reads=[pk], writes=[("cT", s)])
            self.rms_fm(3, 384)
            for h in range(4):
                ps = self.ps[4 + h % 2]
                pk = "ps%d" % (4 + h % 2)
                for c in range(3):
                    P.add("pe", lambda e, h=h, c=c, ps=ps: e.matmul(ps[:], lhsT=self.WQ[:, c, 192 * h:192 * h + 128],
                                                                   rhs=self.cn[:, c, :], start=(c == 0), stop=(c == 2)),
                          reads=[("WQ", c), ("cn", c)], writes=[pk])
                P.add("act", lambda e, h=h, j=j, ps=ps: e.activation(out=self.QN[:, h, j * 512:(j + 1) * 512], in_=ps[:],
                                                                    func=AF.Copy), reads=[pk], writes=[("QN", h, j)])
                for c in range(3):
                    P.add("pe", lambda e, h=h, c=c: e.matmul(self.ps[2][0:64, :],
                                                             lhsT=self.WQ[:, c, 192 * h + 128:192 * h + 192],
                                                             rhs=self.cn[:, c, :], start=(c == 0), stop=(c == 2)),
                          reads=[("WQ", c), ("cn", c)], writes=["ps2"])
                for c in range(3):
                    P.add("pe", lambda e, h=h, c=c: e.matmul(self.ps[3][0:64, :],
                                                             lhsT=self.WQ[:, c, 768 + 64 * h:768 + 64 * h + 64],
                                                             rhs=self.cn[:, c, :], start=(c == 0), stop=(c == 2)),
                          reads=[("WQ", c), ("cn", c)], writes=["ps3"])
                self.rope_combine(self.ps[2], "ps2", self.ps[3], "ps3", self.QPE[0:64, h, j * 512:(j + 1) * 512],
                                  ("QPE", h, j))
            for h in range(4):
                ps = self.ps[4 + h % 2]
                pk = "ps%d" % (4 + h % 2)
                self.fm_matmul(ps, pk, "mg", 128 * h)
                P.add("act", lambda e, h=h, j=j, ps=ps: e.activation(out=self.catT[:, 4 + h, j * 512:(j + 1) * 512],
                                                                    in_=ps[:], func=AF.Silu), reads=[pk],
                      writes=[("catT", 4 + h, j)])
            for s in range(2):
                ps = self.ps[4 + s]
                pk = "ps%d" % (4 + s)
                self.fm_matmul(ps, pk, "pi", 128 * s)
                P.add("act", lambda e, s=s, ps=ps: e.activation(out=self.pT[:, s, :], in_=ps[:], func=AF.Copy),
                      reads=[pk], writes=[("pT", s)])
            for r in range(4):
                m = 4 * j + r
                tc0 = r * 128
                for half, (ps, pk) in enumerate(((self.ps[2], "ps2"), (self.ps[3], "ps3"))):
                    for c in range(8):
                        P.add("pe", lambda e, c=c, ps=ps, half=half, tc0=tc0: e.matmul(
                            ps[:], lhsT=self.hT[:, c, tc0:tc0 + 128], rhs=self.WI[:, c, 512 * half:512 * half + 512],
                            start=(c == 0), stop=(c == 7)), reads=[("WI", c), ("hT", r)], writes=[pk])
                zA, zB = self.ps[2], self.ps[3]
                P.add("act", lambda e: e.activation(out=self.sg[:], in_=zB[:], func=AF.Silu), reads=["ps3"],
                      writes=["sg"])
                P.add("dve", lambda e: e.bn_stats(out=self.st6[:, 0:6], in_=zA[:, 256:512]), reads=["ps2"],
                      writes=["st6"])
                P.add("dve", lambda e: e.bn_aggr(out=self.ss[:, 2:4], in_=self.st6[:, 0:6]), reads=["st6"],
                      writes=["ss2"])
                P.add("dve", lambda e: e.tensor_scalar(out=self.rs[:, 1:2], in0=self.ss[:, 3:4], scalar1=1.0, scalar2=EPS,
                                                       op0=ALU.mult, op1=ALU.add), reads=["ss2"], writes=["rs1"])
                P.add("act", lambda e: e.activation(out=self.rs[:, 1:2], in_=self.rs[:, 1:2], func=AF.Ln), reads=["rs1"],
                      writes=["rs1"])
                P.add("act", lambda e: e.activation(out=self.rs[:, 1:2], in_=self.rs[:, 1:2], func=AF.Exp, scale=-0.5),
                      reads=["rs1"], writes=["rs1"])
                P.add("dve", lambda e: e.tensor_scalar(out=self.vn[:], in0=zA[:, 256:512], scalar1=self.ss[:, 2:3],
                                                       scalar2=self.rs[:, 1:2], op0=ALU.subtract, op1=ALU.mult),
                      reads=["ps2", "ss2", "rs1"], writes=["vn"])
                P.add("dve", lambda e: e.tensor_tensor(out=self.vn[:], in0=self.vn[:], in1=self.lng[:], op=ALU.mult),
                      reads=["vn", "lng"], writes=["vn"])
                P.add("dve", lambda e: e.tensor_tensor(out=self.vln[:], in0=self.vn[:], in1=self.lnb[:], op=ALU.add),
                      reads=["vn", "lnb"], writes=["vln"])
                for h in range(4):
                    P.add("pe", lambda e, h=h: e.matmul(self.ps[7][:, 128 * h:128 * h + 128], lhsT=self.wmT[:, h, :],
                                                        rhs=self.vln[:, 128 * (h // 2):128 * (h // 2) + 128],
                                                        start=True, stop=True),
                          reads=["wmT", "vln"], writes=["ps7"])
                psv7 = self.ps[7][:].rearrange("p (a x) -> p a x", a=2)
                m1v = self.m1[:].rearrange("p (a b d) -> p a b d", a=2, b=2)
                bsv = self.bsf[:].rearrange("p (a b d) -> p a b d", a=2, b=2)
                for b_ in range(2):
                    o_ = 0 if b_ == 0 else 192
                    P.add("dve", lambda e, b_=b_, o_=o_: e.tensor_tensor(out=m1v[:, :, b_, :], in0=psv7[:, :, o_:o_ + 64],
                                                                      in1=bsv[:, :, b_, :], op=ALU.add),
                          reads=["ps7", "bsf"], writes=["m1"])
                P.add("dve", lambda e: e.tensor_tensor(out=self.m1[:], in0=self.m1[:], in1=zA[:, 0:256], op=ALU.mult),
                      reads=["m1", "ps2"], writes=["m1"])
                P.add("dve", lambda e: e.tensor_tensor(out=self.ya[:], in0=self.m1[:], in1=self.sg[:, 0:256],
                                                       op=ALU.mult), reads=["m1", "sg"], writes=["ya"])
                for c in range(2):
                    P.add("pe", lambda e, c=c: e.transpose(out=psT2[:, c, :], in_=self.ya[:, c * 128:(c + 1) * 128],
                                                           identity=self.ident[:]), reads=["ya", "ident"],
                          writes=["ps1"])
                P.add("act", lambda e, m=m: e.activation(out=self.catT[:, 0:2, m * 128:(m + 1) * 128], in_=psT2[:, 0:2, :],
                                                         func=AF.Copy), reads=["ps1"],
                      writes=[("catT", 0, j), ("catT", 1, j)])
                for ch in range(2):
                    P.add("pe", lambda e, ch=ch, tc0=tc0: e.matmul(
                        self.ps[4][:, 128 * ch:128 * ch + 128], lhsT=self.pT[:, ch, tc0:tc0 + 128],
                        rhs=self.wg[:, ch, :], start=True, stop=True), reads=[("pT", ch), "wg"], writes=["ps4"])
                P.add("act", lambda e: e.activation(out=self.pw1[:], in_=self.ps[4][:, 0:256], func=AF.Copy),
                      reads=["ps4"], writes=["pw1"])
                band = self.bandF if m == 0 else self.bandM
                bk = "bandF" if m == 0 else "bandM"
                for g in range(4):
                    c_ = 128 * (g // 2)
                    P.add("pe", lambda e, g=g, band=band, c_=c_: e.matmul(
                        self.ps[6][:, 128 * g:128 * g + 128], lhsT=band[:, g, :], rhs=self.pw1[:, c_:c_ + 128],
                        start=True, stop=False), reads=[bk, "pw1"], writes=["ps6"])
                    P.add("pe", lambda e, g=g, m=m, c_=c_: e.matmul(
                        self.ps[6][:, 128 * g:128 * g + 128], lhsT=self.bandH[:, g, :], rhs=self.halo[:, m, c_:c_ + 128],
                        start=False, stop=True), reads=["bandH", "halo"], writes=["ps6"])
                psv6 = self.ps[6][:].rearrange("p (a x) -> p a x", a=2)
                y1v = self.y1[:].rearrange("p (a b d) -> p a b d", a=2, b=2)
                pscv = self.psc[:].rearrange("p (a b d) -> p a b d", a=2, b=2)
                for b_ in range(2):
                    o_ = 0 if b_ == 0 else 192
                    P.add("dve", lambda e, b_=b_, o_=o_: e.tensor_tensor(out=y1v[:, :, b_, :], in0=psv6[:, :, o_:o_ + 64],
                                                                      in1=pscv[:, :, b_, :], op=ALU.mult),
                          reads=["ps6", "psc"], writes=["y1"])
                P.add("dve", lambda e: e.tensor_tensor(out=self.yb[:], in0=self.y1[:], in1=self.sg[:, 256:512],
                                                       op=ALU.mult), reads=["y1", "sg"], writes=["yb"])
                for c in range(2):
                    P.add("pe", lambda e, c=c: e.transpose(out=psT5[:, c, :], in_=self.yb[:, c * 128:(c + 1) * 128],
                                                           identity=self.ident[:]), reads=["yb", "ident"],
                          writes=["ps5"])
                P.add("act", lambda e, m=m: e.activation(out=self.catT[:, 2:4, m * 128:(m + 1) * 128], in_=psT5[:, 0:2, :],
                                                         func=AF.Copy), reads=["ps5"],
                      writes=[("catT", 2, j), ("catT", 3, j)])

    def phase_B_attn(self, W, C, kv_all, xin, xout):
        P = self.P
        P.barrier()
        a = self.ar
        a.reset()
        KT = a.alloc([SEQ], BF16)
        V = a.alloc([SEQ], BF16)
        KPE = a.alloc([SEQ], BF16)
        PT = [a.alloc([512], BF16) for _ in range(3)]
        rden = [a.alloc([512], F32) for _ in range(2)]
        otmp = [a.alloc([512], F32) for _ in range(2)]
        for mg in range(4):
            P.dma("pool", lambda e, mg=mg: e.dma_start(out=KPE[64:72, mg * 4096:(mg + 1) * 4096], in_=C["umask"]),
                  writes=[("KPE", mg)])
            P.dma("sp", lambda e, mg=mg: e.dma_start(
                out=KPE[0:64, mg * 4096:(mg + 1) * 4096].rearrange("p (i c) -> p i c", i=8),
                in_=kv_all[:, 0:64, 16384 + mg * 512:16384 + (mg + 1) * 512].rearrange("i p c -> p i c")),
                reads=["kv_all"], writes=[("KPE", mg)])

        def load_head(h):
            for mg in (3, 2, 1, 0):
                P.dma("sp", lambda e, h=h, mg=mg: e.dma_start(
                    out=KT[:, mg * 4096:(mg + 1) * 4096].rearrange("p (i c) -> p i c", i=8),
                    in_=kv_all[:, :, h * 2048 + mg * 512:h * 2048 + (mg + 1) * 512].rearrange("i p c -> p i c")),
                    reads=["kv_all"], writes=[("KT", mg)])
                P.dma("pool", lambda e, h=h, mg=mg: e.dma_start(
                    out=V[:, mg * 4096:(mg + 1) * 4096].rearrange("p (i c) -> p i c", i=8),
                    in_=kv_all[:, :, 8192 + h * 2048 + mg * 512:8192 + h * 2048 + (mg + 1) * 512].rearrange(
                        "i p c -> p i c")), reads=["kv_all"], writes=[("V", mg)])

        steps = []
        groups = []
        for h in range(4):
            for j in (3, 2, 1, 0):
                gi = len(groups)
                tl = []
                for r in range(4):
                    for i in range(8):
                        tl.append((32 * j + 4 * i + r, 128 * r, 72, j))
                for g in range(j - 1, -1, -1):
                    for t in range(32):
                        tl.append((32 * g + t, 0, 64, g))
                groups.append((h, j, len(tl)))
                for n_, (T, c0, kd, mg) in enumerate(tl):
                    steps.append((gi, h, j, T, c0, kd, mg, n_ == 0, n_ == len(tl) - 1))
        loaded = set()

        def emit_S(n):
            gi, h, j, T, c0, kd, mg, first, last = steps[n]
            if h not in loaded:
                loaded.add(h)
                load_head(h)
            ps = self.ps[n % 2]
            pk = "ps%d" % (n % 2)
            q0 = j * 512 + c0
            P.add("pe", lambda e: e.matmul(ps[:, c0:512], lhsT=KT[:, T * 128:(T + 1) * 128],
                                           rhs=self.QN[:, h, q0:j * 512 + 512], start=True, stop=False),
                  reads=[("KT", mg), ("QN", h, j)], writes=[pk])
            P.add("pe", lambda e: e.matmul(ps[:, c0:512], lhsT=KPE[0:kd, T * 128:(T + 1) * 128],
                                           rhs=self.QPE[0:kd, h, q0:j * 512 + 512], start=False, stop=True),
                  reads=[("KPE", mg), ("QPE", h, j)], writes=[pk])

        def emit_exp(n):
            gi, h, j, T, c0, kd, mg, first, last = steps[n]
            ps = self.ps[n % 2]
            pk = "ps%d" % (n % 2)
            pt = PT[n % 3]
            P.add("act", lambda e: e.activation(out=pt[:, c0:512], in_=ps[:, c0:512], func=AF.Exp, scale=SCALE),
                  reads=[pk], writes=[("PT", n % 3)])

        def emit_PV(n):
            gi, h, j, T, c0, kd, mg, first, last = steps[n]
            pt = PT[n % 3]
            ob = gi % 2
            pso, psd = self.ps[2 + ob], self.ps[4 + ob]
            ko, kd_ = "ps%d" % (2 + ob), "ps%d" % (4 + ob)
            P.add("pe", lambda e: e.matmul(pso[:, c0:512], lhsT=V[:, T * 128:(T + 1) * 128], rhs=pt[:, c0:512],
                                           start=first, stop=last), reads=[("V", mg), ("PT", n % 3)], writes=[ko])
            P.add("pe", lambda e: e.matmul(psd[:, c0:512], lhsT=self.ones[:], rhs=pt[:, c0:512], start=first,
                                           stop=last), reads=["ones", ("PT", n % 3)], writes=[kd_])
            if last:
                rd, ot = rden[ob], otmp[ob]
                P.add("dve", lambda e: e.reciprocal(out=rd[:], in_=psd[:]), reads=[kd_], writes=[("rden", ob)])
                P.add("dve", lambda e: e.tensor_tensor(out=ot[:], in0=pso[:], in1=rd[:], op=ALU.mult),
                      reads=[ko, ("rden", ob)], writes=[("otmp", ob)])
                P.add("dve", lambda e: e.tensor_tensor(out=self.catT[:, 4 + h, j * 512:(j + 1) * 512], in0=ot[:],
                                                       in1=self.catT[:, 4 + h, j * 512:(j + 1) * 512], op=ALU.mult),
                      reads=[("otmp", ob), ("catT", 4 + h, j)], writes=[("catT", 4 + h, j)])

        N = len(steps)
        for n in range(N):
            emit_S(n)
            emit_exp(n)
            if n >= 1:
                emit_PV(n - 1)
        emit_PV(N - 1)

        P.barrier()
        a.reset()
        WO = a.alloc([8, D], BF16)
        wst = [a.alloc([D], F32), a.alloc([D], F32)]
        pg = a.alloc([D], F32)
        xr = [a.alloc([D], F32), a.alloc([D], F32)]
        ot = [a.alloc([D], F32), a.alloc([D], F32)]
        junk = a.alloc([512], BF16)
        P.dma("pool", lambda e: e.dma_start(out=pg[:], in_=W["post_g"].partition_broadcast(128)), writes=["pg"])
        for c in range(8):
            P.dma(self.dmaq(), lambda e, c=c: e.dma_start(out=wst[c % 2][:], in_=W["w_out"][c * 128:(c + 1) * 128, :]),
                  writes=[("wost", c % 2)])
            if c % 2 == 0:
                P.add("dve", lambda e, c=c: e.tensor_copy(out=WO[:, c, :], in_=wst[c % 2][:]), reads=[("wost", c % 2)],
                      writes=[("WO", c)])
            else:
                P.add("act", lambda e, c=c: e.activation(out=WO[:, c, :], in_=wst[c % 2][:], func=AF.Copy),
                      reads=[("wost", c % 2)], writes=[("WO", c)])
        for m in range(NT):
            j = m // 4
            b = m % 2
            P.dma("sp", lambda e, m=m, b=b: e.dma_start(out=xr[b][:], in_=xin[m * 128:(m + 1) * 128, :]),
                  reads=[(self.xin_key, m)], writes=[("xr", b)])
            for half in range(2):
                ps = self.ps[2 * b + half]
                pk = "ps%d" % (2 * b + half)
                for c in range(8):
                    P.add("pe", lambda e, c=c, m=m, half=half, ps=ps: e.matmul(
                        ps[:], lhsT=self.catT[:, c, m * 128:(m + 1) * 128], rhs=WO[:, c, half * 512:(half + 1) * 512],
                        start=(c == 0), stop=(c == 7)), reads=[("catT", c, j), ("WO", c)], writes=[pk])
                P.add("act", lambda e, half=half, ps=ps: e.activation(out=junk[:], in_=ps[:], func=AF.Square,
                                                                      accum_out=self.ss[:, half:half + 1]),
                      reads=[pk], writes=["junk", ("ssy", half)])
            P.add("dve", lambda e: e.tensor_tensor(out=self.rs[:, 0:1], in0=self.ss[:, 0:1], in1=self.ss[:, 1:2],
                                                   op=ALU.add), reads=[("ssy", 0), ("ssy", 1)], writes=["rsy"])
            P.add("dve", lambda e: e.tensor_scalar(out=self.rs[:, 0:1], in0=self.rs[:, 0:1], scalar1=1.0 / D, scalar2=EPS,
                                                   op0=ALU.mult, op1=ALU.add), reads=["rsy"], writes=["rsy"])
            P.add("act", lambda e: e.activation(out=self.rs[:, 0:1], in_=self.rs[:, 0:1], func=AF.Ln), reads=["rsy"],
                  writes=["rsy"])
            P.add("act", lambda e: e.activation(out=self.rs[:, 0:1], in_=self.rs[:, 0:1], func=AF.Exp, scale=-0.5),
                  reads=["rsy"], writes=["rsy"])
            for half in range(2):
                ps = self.ps[2 * b + half]
                pk = "ps%d" % (2 * b + half)
                P.add("dve", lambda e, half=half, ps=ps, b=b: e.scalar_tensor_tensor(
                    out=ot[b][:, half * 512:(half + 1) * 512], in0=ps[:], scalar=self.rs[:, 0:1],
                    in1=pg[:, half * 512:(half + 1) * 512], op0=ALU.mult, op1=ALU.mult),
                    reads=[pk, "rsy", "pg"], writes=[("ot", b)])
            P.add("pool", lambda e, b=b: e.tensor_tensor(out=ot[b][:], in0=ot[b][:], in1=xr[b][:], op=ALU.add),
                  reads=[("ot", b), ("xr", b)], writes=[("ot", b)])
            P.dma("sp", lambda e, m=m, b=b: e.dma_start(out=xout[m * 128:(m + 1) * 128, :], in_=ot[b][:]),
                  reads=[("ot", b)], writes=[(self.xout_key, m)])


W_SPECS = {
    "pre_g": ([128, 8], F32), "post_g": ([D], F32), "w_in": ([D, DIN], F32), "sgu_wT": ([4, 128, 128], F32),
    "sgu_bT": ([128, 4], F32), "ln_g": ([256], F32), "ln_b": ([256], F32), "pool_wp": ([128, 2, 64], F32),
    "pool_scale": ([256], F32), "q_g": ([128, 3], F32), "w_uq": ([384, 768], F32), "kv_g": ([128, 2], F32),
    "w_ukv": ([256, 1024], F32), "w_out": ([D, D], F32),
}
C_SPECS = {
    "ident": ([128, 128], BF16), "inv_freq": ([64, 1], F32), "bandM": ([128, 4, 128], BF16),
    "bandF": ([128, 4, 128], BF16), "bandH": ([128, 4, 128], BF16), "umask": ([8, 4096], BF16),
    "wmask": ([8, 512], BF16),
}


def build_program(kind):
    nc = bass.Bass("TRN2", target_bir_lowering=False)

    def din(name, shape, dt):
        return nc.dram_tensor(name, shape, dt, kind="ExternalInput").ap()

    W = {k: din(k, s, dt) for k, (s, dt) in W_SPECS.items()}
    C = {k: din(k, s, dt) for k, (s, dt) in C_SPECS.items()}
    xin = din("x_in", [TOK, D], F32)
    pos = din("pos", [TOK], I32)
    with contextlib.ExitStack() as st:
        B = Builder(nc, st)
        B.load_consts(C)
        if kind == "A":
            kv_pay = nc.dram_tensor("kv_pay", [128, KVC], BF16, kind="ExternalOutput").ap()
            tail_pay = nc.dram_tensor("tail_pay", [16, TAILC], BF16, kind="ExternalOutput").ap()
            B.fork().phase_A(W, C, xin, pos, kv_pay, tail_pay)
        else:
            kv_all = din("kv_all", [NCORE, 128, KVC], BF16)
            tail_all = din("tail_all", [NCORE, 16, TAILC], BF16)
            xout = nc.dram_tensor("x_out", [TOK, D], F32, kind="ExternalOutput").ap()
            B.fork().phase_B_front(W, C, xin, pos, tail_all)
            B.fork().phase_B_attn(W, C, kv_all, xin, xout)
        B.P.resolve()
        B.P.emit(nc)
    return nc


def build_fused():
    nc = bass.Bass("TRN2", target_bir_lowering=False)

    def din(name, shape, dt):
        return nc.dram_tensor(name, shape, dt, kind="ExternalInput").ap()

    W2 = {k: din(k, [2] + s_, dt) for k, (s_, dt) in W_SPECS.items()}
    C = {k: din(k, s_, dt) for k, (s_, dt) in C_SPECS.items()}
    xin = din("x_in", [TOK, D], F32)
    pos = din("pos", [TOK], I32)
    xout = nc.dram_tensor("x_out", [TOK, D], F32, kind="ExternalOutput").ap()
    xs = nc.dram_tensor("x_mid", [TOK, D], F32, kind="ExternalOutput" if DEBUG_STAGE == 77 else "Internal").ap()
    kv_pay = nc.dram_tensor("kv_pay", [128, KVC], BF16, kind="Internal").ap()
    tail_pay = nc.dram_tensor("tail_pay", [16, TAILC], BF16, kind="Internal").ap()
    kv_all2 = nc.dram_tensor("kv_all", [NCORE * 128, KVC], BF16, kind="Internal", addr_space="Local").ap()
    tail_all2 = nc.dram_tensor("tail_all", [NCORE * 16, TAILC], BF16, kind="Internal", addr_space="Local").ap()
    kv_all = kv_all2.rearrange("(i p) c -> i p c", i=NCORE)
    tail_all = tail_all2.rearrange("(i s) c -> i s c", i=NCORE)
    with contextlib.ExitStack() as st:
        B = Builder(nc, st)
        B.load_consts(C)
        for l in range(2):
            W = {k: v[l] for k, v in W2.items()}
            x_l = xin if l == 0 else xs
            B.xin_key = "xin" if l == 0 else "xmid"
            B.xout_key = "xmid" if l == 0 else "xout"
            B.fork().phase_A(W, C, x_l, pos, kv_pay, tail_pay)
            B.fork().allgather(kv_pay, tail_pay, kv_all2, tail_all2)
            B.fork().phase_B_front(W, C, x_l, pos, tail_all)
            B.fork().phase_B_attn(W, C, kv_all, x_l, xs if l == 0 else xout)
        B.P.resolve()
        B.P.emit(nc)
    return nc


def host_consts():
    bf = ml_dtypes.bfloat16
    ident = np.eye(128, dtype=np.float32).astype(bf)
    f = np.power(np.float32(10000.0), -(np.arange(0, 64, 2, dtype=np.float32) / np.float32(64.0))).astype(np.float32)
    inv_freq = np.concatenate([f, f]).reshape(64, 1).astype(np.float32)
    wins = (2, 4, 8, 16)
    t = np.arange(128)[:, None]
    s = np.arange(128)[None, :]
    bandM = np.zeros((128, 4, 128), np.float32)
    bandF = np.zeros((128, 4, 128), np.float32)
    bandHb = np.zeros((16, 4, 128), np.float32)
    for g, w in enumerate(wins):
        inwin = ((t - s) >= 0) & ((t - s) <= w - 1)
        Bm = inwin * (1.0 / w) - (t == s) * 1.0
        cnt = np.minimum(t + 1, w).astype(np.float32)
        Bf = inwin * (1.0 / cnt) - (t == s) * 1.0
        bandM[:, g, :] = Bm.T
        bandF[:, g, :] = Bf.T
        s2 = np.arange(16)[None, :]
        inh = (t + 16 - s2) <= (w - 1)
        bandHb[:, g, :] = (inh * (1.0 / w)).T
    wmask = np.zeros((8, 512), np.float32)
    for rho in range(8):
        wmask[rho, 64 * rho:64 * rho + 64] = 1.0
    per_core = []
    for ic in range(NCORE):
        bandH = np.zeros((128, 4, 128), np.float32)
        bandH[16 * ic:16 * ic + 16] = bandHb
        um = np.zeros((8, 4096), np.float32)
        for i in range(8):
            for r in range(4):
                for hk in range(2):
                    ckey = 2 * (8 * r + i) + hk
                    k0 = (4 * i + r) * 128 + 64 * hk
                    for rho in range(8):
                        cq = 16 * (rho // 2) + 2 * ic + (rho % 2)
                        if ckey > cq:
                            um[rho, k0:k0 + 64] = -BIG
        per_core.append({
            "ident": ident, "inv_freq": inv_freq, "bandM": bandM.astype(bf),
            "bandF": (bandF if ic == 0 else bandM).astype(bf), "bandH": bandH.astype(bf),
            "umask": um.astype(bf), "wmask": wmask.astype(bf),
        })
    return per_core


def host_weights(inp, l):
    c = np.ascontiguousarray
    return {
        "pre_g": c(inp["pre_norm_g"][l].reshape(8, 128).T), "post_g": c(inp["post_norm_g"][l]),
        "w_in": c(inp["w_in"][l]), "sgu_wT": c(inp["sgu_w"][l].transpose(0, 2, 1)),
        "sgu_bT": c(inp["sgu_b"][l].T), "ln_g": c(inp["sgu_ln_g"][l]), "ln_b": c(inp["sgu_ln_b"][l]),
        "pool_wp": c(inp["pool_w"][l].reshape(2, 2, 64, 64).transpose(1, 2, 0, 3).reshape(128, 2, 64)),
        "pool_scale": c(inp["pool_scale"][l]), "q_g": c(inp["q_norm_g"][l].reshape(3, 128).T),
        "w_uq": c(inp["w_uq"][l]), "kv_g": c(inp["kv_norm_g"][l].reshape(2, 128).T), "w_ukv": c(inp["w_ukv"][l]),
        "w_out": c(inp["w_out"][l]),
    }


_PROGS = {}


def _prog(kind):
    if kind not in _PROGS:
        _PROGS[kind] = build_fused() if kind == "F" else build_program(kind)
    return _PROGS[kind]


def kernel(**inputs):
    inp = {k: np.asarray(v) for k, v in inputs.items()}
    x = inp["x"][0].astype(np.float32, copy=False)
    pos = inp["positions"][0].astype(np.int32, copy=False)
    xs = x.reshape(NT, NCORE, 128, D)
    ps_ = pos.reshape(NT, NCORE, 128)
    x_own = [np.ascontiguousarray(xs[:, i]).reshape(TOK, D) for i in range(NCORE)]
    pos_own = [np.ascontiguousarray(ps_[:, i]).reshape(TOK) for i in range(NCORE)]
    consts = host_consts()
    cores = list(range(NCORE))
    w0, w1 = host_weights(inp, 0), host_weights(inp, 1)
    W2 = {k: np.ascontiguousarray(np.stack([w0[k], w1[k]])) for k in w0}
    nc = _prog("F")
    res = run_bass_kernel_spmd(nc, [dict(W2, **consts[i], x_in=x_own[i], pos=pos_own[i]) for i in cores], core_ids=cores)
    out = np.empty((NT, NCORE, 128, D), np.float32)
    for i in cores:
        out[:, i] = np.asarray(res.results[i]["x_out"]).reshape(NT, 128, D)
    return out.reshape(1, SEQ, D)
```
